# Optimizing a Trainium2 kernel written in Bass

```python
import math
import numpy as np
import jax
import jax.numpy as jnp
from jax import lax

D_MODEL = 2048
BATCH = 4
SEQ = 2048
DEPTH = 2
DEC_BATCH = 1
DEC_SEQ = 16384
PAST_LEN = 128

LRU_WIDTH = D_MODEL // 2
LRU_BLOCKS = 16
LRU_BLOCK_W = LRU_WIDTH // LRU_BLOCKS
LRU_CONV_W = 4
LRU_C = 8.0
S5_WIDTH = D_MODEL // 2
S5_GROUP_CH = 16
S5_GROUPS = S5_WIDTH // S5_GROUP_CH
S5_STATE = 64
HEAD_DIM = 64
ATT_GROUPS = ((128, 1), (512, 4), (2048, 16))
N_ATT_GROUPS = 3
ATT_WIDTH = 3 * D_MODEL // 4
ATT_HEADS = ATT_WIDTH // HEAD_DIM
HEADS_PER_GROUP = ATT_HEADS // N_ATT_GROUPS
ATT_OUT = HEADS_PER_GROUP * HEAD_DIM
Q_BLOCK = 128
REL_BUCKETS = 32
REL_MAX_DIST = 1024
D_FF = 11 * D_MODEL // 4
N_NORMS = 6
N_IN = 2 * LRU_WIDTH + S5_WIDTH + 3 * ATT_WIDTH + 3 * D_MODEL
RMS_EPS = 1e-6
NEG_INF = -1e30

kernel_name = 'hybrid_bidir_rglru_s5_dilated_attn_encoder'


def _t5_buckets(rel):
    half = REL_BUCKETS // 2
    max_exact = half // 2
    sign = (rel > 0).astype(np.int32) * half
    n = np.abs(rel)
    large = max_exact + (np.log(np.maximum(n, 1) / max_exact)
                         / np.log(REL_MAX_DIST / max_exact) * (half - max_exact)).astype(np.int32)
    large = np.minimum(large, half - 1)
    return sign + np.where(n < max_exact, n, large)


def _rmsnorm(x, g):
    x32 = x.astype(jnp.float32)
    y = x32 * lax.rsqrt(jnp.mean(x32 * x32, axis=-1, keepdims=True) + RMS_EPS)
    return (y * g.astype(jnp.float32)).astype(x.dtype)


def _swiglu(h, w1, w3, w2):
    return (jax.nn.silu(h @ w1) * (h @ w3)) @ w2


def _centred_dwconv(x, w, b):
    T = x.shape[1]
    left = LRU_CONV_W // 2
    xp = jnp.pad(x, ((0, 0), (left, LRU_CONV_W - 1 - left), (0, 0)))
    y = b
    for kk in range(LRU_CONV_W):
        y = y + xp[:, kk:kk + T] * w[kk]
    return y


def _linear_combine(e1, e2):
    a1, b1 = e1
    a2, b2 = e2
    return a1 * a2, a2 * b1 + b2


def _complex_combine(e1, e2):
    ar1, ai1, br1, bi1 = e1
    ar2, ai2, br2, bi2 = e2
    return (ar2 * ar1 - ai2 * ai1, ar2 * ai1 + ai2 * ar1,
            ar2 * br1 - ai2 * bi1 + br2, ar2 * bi1 + ai2 * br1 + bi2)


def _rglru_direction(xc, wa, ba, wx, bx, lam, reverse):
    B, T, W = xc.shape
    xb = xc.reshape(B, T, LRU_BLOCKS, LRU_BLOCK_W)
    r = jax.nn.sigmoid(jnp.einsum('btni,nij->btnj', xb, wa).reshape(B, T, W) + ba)
    i = jax.nn.sigmoid(jnp.einsum('btni,nij->btnj', xb, wx).reshape(B, T, W) + bx)
    log_a = -LRU_C * r * jax.nn.softplus(-lam.astype(jnp.float32))
    a = jnp.exp(log_a)
    b = jnp.sqrt(-jnp.expm1(2.0 * log_a)) * (i * xc)
    _, h = lax.associative_scan(_linear_combine, (a, b), axis=1, reverse=reverse)
    return h


def _s5_direction(u, lam_re, lam_im, log_dt, b_re, b_im, c_re, c_im, reverse):
    B, T, _ = u.shape
    lr = lam_re.astype(jnp.float32)
    li = lam_im.astype(jnp.float32)
    dt = jnp.exp(log_dt.astype(jnp.float32))[:, None]
    mag = jnp.exp(lr * dt)
    ar = mag * jnp.cos(li * dt)
    ai = mag * jnp.sin(li * dt)
    den = lr * lr + li * li
    cr = ((ar - 1.0) * lr + ai * li) / den
    ci = (ai * lr - (ar - 1.0) * li) / den
    bbr = cr[..., None] * b_re - ci[..., None] * b_im
    bbi = cr[..., None] * b_im + ci[..., None] * b_re
    ug = u.reshape(B, T, S5_GROUPS, S5_GROUP_CH)
    bur = jnp.einsum('btgc,gpc->btgp', ug, bbr)
    bui = jnp.einsum('btgc,gpc->btgp', ug, bbi)
    arb = jnp.broadcast_to(ar, bur.shape)
    aib = jnp.broadcast_to(ai, bur.shape)
    _, _, xr, xi = lax.associative_scan(_complex_combine, (arb, aib, bur, bui), axis=1, reverse=reverse)
    y = jnp.einsum('btgp,gcp->btgc', xr, c_re) - jnp.einsum('btgp,gcp->btgc', xi, c_im)
    return y.reshape(B, T, S5_WIDTH)


def _dilated_attention(q, k, v, rel_bias):
    B, T, H, Dh = q.shape
    n_blk = T // Q_BLOCK
    q = q * (HEAD_DIM ** -0.5)
    specs = []
    for g, (window, dil) in enumerate(ATT_GROUPS):
        n_side = (window // 2) // dil
        offs = dil * np.arange(-n_side, n_side + 1)
        hs = slice(g * HEADS_PER_GROUP, (g + 1) * HEADS_PER_GROUP)
        bias = rel_bias[_t5_buckets(offs)][:, hs].T.astype(jnp.float32)
        specs.append((offs, bias, q[:, :, hs], k[:, :, hs], v[:, :, hs]))

    def block(i):
        pos = i * Q_BLOCK + jnp.arange(Q_BLOCK)
        outs, lses = [], []
        for offs, bias, qg, kg, vg in specs:
            idx = pos[:, None] + offs[None, :]
            valid = (idx >= 0) & (idx < T)
            idx = jnp.clip(idx, 0, T - 1)
            qb = lax.dynamic_slice_in_dim(qg, i * Q_BLOCK, Q_BLOCK, axis=1)
            kb = jnp.take(kg, idx, axis=1)
            vb = jnp.take(vg, idx, axis=1)
            s = jnp.einsum('bqhd,bqjhd->bhqj', qb, kb).astype(jnp.float32) + bias[None, :, None, :]
            s = jnp.where(valid[None, None], s, NEG_INF)
            lse = jax.nn.logsumexp(s, axis=-1)
            p = jnp.exp(s - lse[..., None])
            outs.append(jnp.einsum('bhqj,bqjhd->bqhd', p, vb.astype(jnp.float32)))
            lses.append(lse)
        wts = jnp.transpose(jax.nn.softmax(jnp.stack(lses, axis=-1), axis=-1), (0, 2, 1, 3))
        o = jnp.sum(jnp.stack(outs, axis=-1) * wts[..., None, :], axis=-1)
        return o.reshape(B, Q_BLOCK, ATT_OUT).astype(q.dtype)

    y = lax.map(block, jnp.arange(n_blk))
    return jnp.transpose(y, (1, 0, 2, 3)).reshape(B, T, ATT_OUT)


def _mixer(h, w_in, conv_w, conv_b, lru_wa, lru_ba, lru_wx, lru_bx, lru_L,
           lam_re, lam_im, log_dt, b_re, b_im, c_re, c_im, s5_d, glu_w, glu_b,
           w_br_lru, w_br_s5, w_br_att, w_out, rel_bias):
    B, T, _ = h.shape
    z = h @ w_in
    cuts = np.cumsum([LRU_WIDTH, LRU_WIDTH, S5_WIDTH, ATT_WIDTH, ATT_WIDTH, ATT_WIDTH, D_MODEL, D_MODEL])
    xl, gl, u, q, k, v, ga, gb, gc = jnp.split(z, [int(c) for c in cuts], axis=-1)
    xc = _centred_dwconv(xl, conv_w, conv_b).astype(jnp.float32)
    y_lru = (_rglru_direction(xc, lru_wa[0], lru_ba[0], lru_wx[0], lru_bx[0], lru_L[0], False)
             + _rglru_direction(xc, lru_wa[1], lru_ba[1], lru_wx[1], lru_bx[1], lru_L[1], True))
    y_lru = y_lru.astype(h.dtype) * jax.nn.gelu(gl)
    u32 = u.astype(jnp.float32)
    y = (_s5_direction(u32, lam_re[0], lam_im[0], log_dt[0], b_re[0], b_im[0], c_re[0], c_im[0], False)
         + _s5_direction(u32, lam_re[1], lam_im[1], log_dt[1], b_re[1], b_im[1], c_re[1], c_im[1], True)
         + s5_d * u32)
    y1 = jax.nn.gelu(y)
    y_s5 = (y1 * jax.nn.sigmoid(y1 @ glu_w + glu_b)).astype(h.dtype)
    y_att = _dilated_attention(q.reshape(B, T, ATT_HEADS, HEAD_DIM), k.reshape(B, T, ATT_HEADS, HEAD_DIM),
                               v.reshape(B, T, ATT_HEADS, HEAD_DIM), rel_bias)
    m = (jax.nn.sigmoid(ga) * (y_lru @ w_br_lru)
         + jax.nn.sigmoid(gb) * (y_s5 @ w_br_s5)
         + jax.nn.sigmoid(gc) * (y_att @ w_br_att))
    return m @ w_out


def _trunk(x, params):
    (norm_g, w_in, lru_conv_w, lru_conv_b, lru_wa, lru_ba, lru_wx, lru_bx, lru_L,
     s5_lam_re, s5_lam_im, s5_log_dt, s5_b_re, s5_b_im, s5_c_re, s5_c_im, s5_d, s5_glu_w, s5_glu_b,
     rel_bias, w_br_lru, w_br_s5, w_br_att, w_out, ffn_w1, ffn_w3, ffn_w2) = params
    for l in range(DEPTH):
        g = norm_g[l]
        h = _rmsnorm(x, g[0])
        x = x + 0.5 * _rmsnorm(_swiglu(h, ffn_w1[l, 0], ffn_w3[l, 0], ffn_w2[l, 0]), g[1])
        h = _rmsnorm(x, g[2])
        mix = _mixer(h, w_in[l], lru_conv_w[l], lru_conv_b[l], lru_wa[l], lru_ba[l], lru_wx[l], lru_bx[l], lru_L[l],
                     s5_lam_re[l], s5_lam_im[l], s5_log_dt[l], s5_b_re[l], s5_b_im[l], s5_c_re[l], s5_c_im[l],
                     s5_d[l], s5_glu_w[l], s5_glu_b[l], w_br_lru[l], w_br_s5[l], w_br_att[l], w_out[l], rel_bias)
        x = x + _rmsnorm(mix, g[3])
        h = _rmsnorm(x, g[4])
        x = x + 0.5 * _rmsnorm(_swiglu(h, ffn_w1[l, 1], ffn_w3[l, 1], ffn_w2[l, 1]), g[5])
    return x


def setup_inputs(seed: int = 0) -> dict:
    key = jax.random.key(seed)
    ks = jax.random.split(key, 32)
    f32 = jnp.float32
    L, D = DEPTH, D_MODEL

    def nrm(k, shape, scale):
        return jax.random.normal(k, shape, f32) * scale

    x_prompt = nrm(ks[0], (BATCH, SEQ, D), 1.0)
    x_sample = nrm(ks[1], (DEC_BATCH, DEC_SEQ, D), 1.0)
    norm_g = 1.0 + nrm(ks[2], (L, N_NORMS, D), 0.02)
    w_in = nrm(ks[3], (L, D, N_IN), D ** -0.5)
    lru_conv_w = nrm(ks[4], (L, LRU_CONV_W, LRU_WIDTH), LRU_CONV_W ** -0.5)
    lru_conv_b = nrm(ks[5], (L, LRU_WIDTH), 0.01)
    lru_wa = nrm(ks[6], (L, 2, LRU_BLOCKS, LRU_BLOCK_W, LRU_BLOCK_W), LRU_BLOCK_W ** -0.5)
    lru_ba = nrm(ks[7], (L, 2, LRU_WIDTH), 0.01)
    lru_wx = nrm(ks[8], (L, 2, LRU_BLOCKS, LRU_BLOCK_W, LRU_BLOCK_W), LRU_BLOCK_W ** -0.5)
    lru_bx = nrm(ks[9], (L, 2, LRU_WIDTH), 0.01)
    a_init = jax.random.uniform(ks[10], (L, 2, LRU_WIDTH), f32, 0.9, 0.999) ** (1.0 / LRU_C)
    lru_L = jnp.log(a_init) - jnp.log1p(-a_init)
    n_idx = jnp.arange(S5_STATE, dtype=f32)
    s5_lam_re = -0.5 + nrm(ks[11], (L, 2, S5_GROUPS, S5_STATE), 0.02)
    s5_lam_im = jnp.pi * n_idx + nrm(ks[12], (L, 2, S5_GROUPS, S5_STATE), 0.02)
    s5_log_dt = jax.random.uniform(ks[13], (L, 2, S5_GROUPS), f32, math.log(1e-3), math.log(1e-1))
    s5_b_re = nrm(ks[14], (L, 2, S5_GROUPS, S5_STATE, S5_GROUP_CH), (2 * S5_GROUP_CH) ** -0.5)
    s5_b_im = nrm(ks[15], (L, 2, S5_GROUPS, S5_STATE, S5_GROUP_CH), (2 * S5_GROUP_CH) ** -0.5)
    s5_c_re = nrm(ks[16], (L, 2, S5_GROUPS, S5_GROUP_CH, S5_STATE), 0.5)
    s5_c_im = nrm(ks[17], (L, 2, S5_GROUPS, S5_GROUP_CH, S5_STATE), 0.5)
    s5_d = nrm(ks[18], (L, S5_WIDTH), 1.0)
    s5_glu_w = nrm(ks[19], (L, S5_WIDTH, S5_WIDTH), S5_WIDTH ** -0.5)
    s5_glu_b = nrm(ks[20], (L, S5_WIDTH), 0.01)
    rel_bias = nrm(ks[21], (REL_BUCKETS, ATT_HEADS), 0.5)
    w_br_lru = nrm(ks[22], (L, LRU_WIDTH, D), LRU_WIDTH ** -0.5)
    w_br_s5 = nrm(ks[23], (L, S5_WIDTH, D), S5_WIDTH ** -0.5)
    w_br_att = nrm(ks[24], (L, ATT_OUT, D), ATT_OUT ** -0.5)
    w_out = nrm(ks[25], (L, D, D), D ** -0.5)
    ffn_w1 = nrm(ks[26], (L, 2, D, D_FF), D ** -0.5)
    ffn_w3 = nrm(ks[27], (L, 2, D, D_FF), D ** -0.5)
    ffn_w2 = nrm(ks[28], (L, 2, D_FF, D), D_FF ** -0.5)
    return {'x_prompt': x_prompt, 'x_sample': x_sample, 'norm_g': norm_g, 'w_in': w_in,
            'lru_conv_w': lru_conv_w, 'lru_conv_b': lru_conv_b, 'lru_wa': lru_wa, 'lru_ba': lru_ba,
            'lru_wx': lru_wx, 'lru_bx': lru_bx, 'lru_L': lru_L,
            's5_lam_re': s5_lam_re, 's5_lam_im': s5_lam_im, 's5_log_dt': s5_log_dt,
            's5_b_re': s5_b_re, 's5_b_im': s5_b_im, 's5_c_re': s5_c_re, 's5_c_im': s5_c_im,
            's5_d': s5_d, 's5_glu_w': s5_glu_w, 's5_glu_b': s5_glu_b, 'rel_bias': rel_bias,
            'w_br_lru': w_br_lru, 'w_br_s5': w_br_s5, 'w_br_att': w_br_att, 'w_out': w_out,
            'ffn_w1': ffn_w1, 'ffn_w3': ffn_w3, 'ffn_w2': ffn_w2}


def reference(x_prompt, x_sample, norm_g, w_in, lru_conv_w, lru_conv_b, lru_wa, lru_ba, lru_wx, lru_bx, lru_L,
              s5_lam_re, s5_lam_im, s5_log_dt, s5_b_re, s5_b_im, s5_c_re, s5_c_im, s5_d, s5_glu_w, s5_glu_b,
              rel_bias, w_br_lru, w_br_s5, w_br_att, w_out, ffn_w1, ffn_w3, ffn_w2):
    params = (norm_g, w_in, lru_conv_w, lru_conv_b, lru_wa, lru_ba, lru_wx, lru_bx, lru_L,
              s5_lam_re, s5_lam_im, s5_log_dt, s5_b_re, s5_b_im, s5_c_re, s5_c_im, s5_d, s5_glu_w, s5_glu_b,
              rel_bias, w_br_lru, w_br_s5, w_br_att, w_out, ffn_w1, ffn_w3, ffn_w2)
    y_prompt = _trunk(x_prompt, params)
    y_sample = _trunk(x_sample, params)
    return (y_prompt, y_sample)
```

```python
import os

import numpy as np
from contextlib import ExitStack
import concourse.bass as bass
import concourse.mybir as mybir
from concourse.bass_utils import run_bass_kernel_spmd

F32 = mybir.dt.float32
BF16 = mybir.dt.bfloat16
I32 = mybir.dt.int32
ALU = mybir.AluOpType
AF = mybir.ActivationFunctionType
AX = mybir.AxisListType

SAME_ENGINE_SYNC = True


class Trk:
    __slots__ = ("w", "r")

    def __init__(self):
        self.w = None
        self.r = {}


class KB:
    def __init__(self, nc, es, ndma=8):
        self.nc = nc
        self.es = es
        self.engs = {"pe": nc.tensor, "act": nc.scalar, "dve": nc.vector, "pool": nc.gpsimd, "sp": nc.sync}
        self.sems = {}
        self.cnt = {}
        for n in self.engs:
            self._new_sem(n)
        self.known = {e: {} for e in self.engs}
        self.ndma = ndma
        self.dslots = {}
        self.dma_i = {}
        for q in ("sp", "pool", "act"):
            self.dslots[q] = [self._new_sem(f"dq_{q}{i}") for i in range(ndma)]
            self.dma_i[q] = 0
        self.uid = 0

    def _new_sem(self, name):
        h = self.es.enter_context(self.nc.semaphore(name))
        self.sems[name] = h
        self.cnt[name] = 0
        return name

    def sb(self, shape, dt, name=None):
        self.uid += 1
        return self.es.enter_context(self.nc.sbuf_tensor(f"sb{self.uid}_{name or ''}", list(shape), dt))

    def ps(self, shape, dt, name=None):
        self.uid += 1
        return self.es.enter_context(self.nc.psum_tensor(f"ps{self.uid}_{name or ''}", list(shape), dt))

    def _deps(self, reads, writes):
        deps = {}
        for t in reads:
            if t.w is not None:
                s, v = t.w
                if deps.get(s, 0) < v:
                    deps[s] = v
        for t in writes:
            if t.w is not None:
                s, v = t.w
                if deps.get(s, 0) < v:
                    deps[s] = v
            for s, v in t.r.items():
                if deps.get(s, 0) < v:
                    deps[s] = v
        return deps

    def _wait(self, e, deps):
        kn = self.known[e]
        for s, v in deps.items():
            if s == e and (e == "pe" or not SAME_ENGINE_SYNC):
                continue
            if kn.get(s, 0) >= v:
                continue
            self.engs[e].wait_ge(self.sems[s], v)
            kn[s] = v

    def _stamp(self, me, reads, writes):
        s, v = me
        for t in reads:
            if t.r.get(s, 0) < v:
                t.r[s] = v
        for t in writes:
            t.w = me
            t.r = {}

    def op(self, e, fn, reads=(), writes=(), inc=True):
        self._wait(e, self._deps(reads, writes))
        ins = fn(self.engs[e])
        if inc:
            self.cnt[e] += 1
            ins.then_inc(self.sems[e], 1)
            me = (e, self.cnt[e])
        else:
            me = (e, self.cnt[e] + 1)
        self._stamp(me, reads, writes)
        return ins

    def dma(self, q, out, in_, reads=(), writes=()):
        i = self.dma_i[q]
        self.dma_i[q] += 1
        s = self.dslots[q][i % self.ndma]
        deps = self._deps(reads, writes)
        if self.cnt[s] > 0 and deps.get(s, 0) < self.cnt[s]:
            deps[s] = self.cnt[s]
        self._wait(q, deps)
        ins = self.engs[q].dma_start(out=out, in_=in_)
        self.cnt[s] += 16
        ins.then_inc(self.sems[s], 16)
        self._stamp((s, self.cnt[s]), reads, writes)
        return ins

    def barrier(self):
        allv = {s: v for s, v in self.cnt.items() if v > 0}
        for e in self.engs:
            self._wait(e, dict(allv))


D = 2048
KT = D // 128
DFF = 5632
JT = DFF // 128
NB = 512


class Env:
    pass


def load_consts(kb, env, cdram):
    nc = kb.nc
    env.ones = kb.sb([128, 128], BF16, "ones"); env.t_ones = Trk()
    kb.op("dve", lambda e: e.memset(env.ones[:, :], 1.0), writes=[env.t_ones])
    env.identf = kb.sb([128, 128], F32, "identf"); env.t_identf = Trk()
    kb.dma("sp", env.identf[:, :], cdram["ident"][:, :], writes=[env.t_identf])
    ncol = cdram["gcol"].shape[1]
    env.gcol = kb.sb([128, ncol], F32, "gcol"); env.t_gcol = Trk()
    kb.dma("sp", env.gcol[:, :], cdram["gcol"][:, :], writes=[env.t_gcol])
    env.ghalf = kb.sb([128, ncol], F32, "ghalf"); env.t_ghalf = Trk()
    kb.op("dve", lambda e: e.tensor_scalar(env.ghalf[:, :], env.gcol[:, :], 0.5, None, ALU.mult), reads=[env.t_gcol], writes=[env.t_ghalf])
    env.psbig = kb.ps([128, 8, NB], F32, "psbig")
    env.ps = [env.psbig[:, i, :] for i in range(8)]
    env.t_ps = [Trk() for _ in range(8)]


def transpose_in(kb, env, x_tok, xT, T):
    es = ExitStack()
    with es:
        old = kb.es; kb.es = es
        xin = [kb.sb([128, D], F32) for _ in range(2)]; t_xin = [Trk(), Trk()]
        xo = [kb.sb([128, KT, 128], F32) for _ in range(2)]; t_xo = [Trk(), Trk()]
        for b in range(T // 128):
            s = b % 2
            kb.dma("sp", xin[s][:, :], x_tok[b * 128:(b + 1) * 128, :], writes=[t_xin[s]])
            for q in range(4):
                pb = env.ps[(b % 2) * 4 + q]; tp = env.t_ps[(b % 2) * 4 + q]
                for i in range(4):
                    k = q * 4 + i
                    kb.op("pe", lambda e: e.transpose(pb[:, i * 128:(i + 1) * 128], xin[s][:, k * 128:(k + 1) * 128], env.identf[:, :]),
                          reads=[t_xin[s], env.t_identf], writes=[tp], inc=(i == 3))
                eng = "act" if q % 2 == 0 else "dve"
                if eng == "act":
                    kb.op("act", lambda e: e.copy(xo[s][:, q * 4:(q + 1) * 4, :], pb[:, :].rearrange("p (a b) -> p a b", a=4)), reads=[tp], writes=[t_xo[s]])
                else:
                    kb.op("dve", lambda e: e.tensor_copy(xo[s][:, q * 4:(q + 1) * 4, :], pb[:, :].rearrange("p (a b) -> p a b", a=4)), reads=[tp], writes=[t_xo[s]])
            kb.dma("sp", xT[:, b * 128:(b + 1) * 128].rearrange("(k p) t -> p k t", p=128), xo[s][:, :, :], reads=[t_xo[s]])
        kb.barrier()
        kb.es = old


def transpose_out(kb, env, xT, y_tok, T):
    es = ExitStack()
    with es:
        old = kb.es; kb.es = es
        xin = [kb.sb([128, KT, 128], F32) for _ in range(2)]; t_xin = [Trk(), Trk()]
        xo = [kb.sb([128, D], F32) for _ in range(2)]; t_xo = [Trk(), Trk()]
        for b in range(T // 128):
            s = b % 2
            kb.dma("sp", xin[s][:, :, :], xT[:, b * 128:(b + 1) * 128].rearrange("(k p) t -> p k t", p=128), writes=[t_xin[s]])
            for q in range(4):
                pb = env.ps[(b % 2) * 4 + q]; tp = env.t_ps[(b % 2) * 4 + q]
                for i in range(4):
                    k = q * 4 + i
                    kb.op("pe", lambda e: e.transpose(pb[:, i * 128:(i + 1) * 128], xin[s][:, k, :], env.identf[:, :]),
                          reads=[t_xin[s], env.t_identf], writes=[tp], inc=(i == 3))
                if q % 2 == 0:
                    kb.op("act", lambda e: e.copy(xo[s][:, q * 512:(q + 1) * 512], pb[:, :]), reads=[tp], writes=[t_xo[s]])
                else:
                    kb.op("dve", lambda e: e.tensor_copy(xo[s][:, q * 512:(q + 1) * 512], pb[:, :]), reads=[tp], writes=[t_xo[s]])
            kb.dma("sp", y_tok[b * 128:(b + 1) * 128, :], xo[s][:, :], reads=[t_xo[s]])
        kb.barrier()
        kb.es = old


def rms_stats(kb, env, src, t_src, sq, t_sq, rstd, t_rstd, psi, nt=KT, dim=D):
    kb.op("act", lambda e: e.activation(sq[:, 0:nt, :], src[:, 0:nt, :], AF.Square), reads=[t_src], writes=[t_sq])
    pb = env.ps[psi]; tp = env.t_ps[psi]
    for k in range(nt):
        kb.op("pe", lambda e: e.matmul(pb[:, :], env.ones[:, :], sq[:, k, :], start=(k == 0), stop=(k == nt - 1)),
              reads=[env.t_ones, t_sq], writes=[tp], inc=(k == nt - 1))
    kb.op("act", lambda e: e.activation(rstd[:, :], pb[:, :], AF.Sqrt, bias=env.eps[:, 0:1], scale=1.0 / dim), reads=[tp, env.t_eps], writes=[t_rstd])
    kb.op("dve", lambda e: e.reciprocal(rstd[:, :], rstd[:, :]), reads=[t_rstd], writes=[t_rstd])


def ffn_stage(kb, env, xTi, xTo, T, w1, w3, w2, gc_a, gc_b):
    es = ExitStack()
    with es:
        old = kb.es; kb.es = es
        xs = kb.sb([128, KT, NB], F32); t_xs = Trk()
        ys = kb.sb([128, KT, NB], F32); t_ys = Trk()
        hT = kb.sb([128, KT, NB], BF16); t_hT = Trk()
        gate = kb.sb([128, JT, NB], BF16); t_gate = Trk()
        rstd = kb.sb([128, NB], F32); t_rstd = Trk()
        rstd2 = kb.sb([128, NB], F32); t_rstd2 = Trk()
        w1s = [kb.sb([128, KT, 128], BF16) for _ in range(3)]; t_w1s = [Trk(), Trk(), Trk()]
        w3s = [kb.sb([128, KT, 128], BF16) for _ in range(3)]; t_w3s = [Trk(), Trk(), Trk()]
        w2s = [kb.sb([128, JT, 128], BF16) for _ in range(2)]; t_w2s = [Trk(), Trk()]
        sil = [kb.sb([128, NB], F32) for _ in range(2)]; t_sil = [Trk(), Trk()]
        ot = [kb.sb([128, NB], F32) for _ in range(2)]; t_ot = [Trk(), Trk()]
        for blk in range(T // NB):
            c0 = blk * NB
            kb.dma("sp", xs[:, :, :], xTi[:, c0:c0 + NB].rearrange("(k p) t -> p k t", p=128), writes=[t_xs])
            rms_stats(kb, env, xs, t_xs, gate, t_gate, rstd, t_rstd, 7)
            for k in range(KT):
                kb.op("dve", lambda e: e.scalar_tensor_tensor(hT[:, k, :], xs[:, k, :], env.gcol[:, gc_a + k:gc_a + k + 1], rstd[:, :], ALU.mult, ALU.mult),
                      reads=[t_xs, env.t_gcol, t_rstd], writes=[t_hT])
            for j in range(JT):
                s = j % 3
                kb.dma("pool", w1s[s][:, :, :], w1[j], writes=[t_w1s[s]])
                kb.dma("pool", w3s[s][:, :, :], w3[j], writes=[t_w3s[s]])
                pa = env.ps[2 * (j % 2)]; ta = env.t_ps[2 * (j % 2)]
                pb = env.ps[2 * (j % 2) + 1]; tb = env.t_ps[2 * (j % 2) + 1]
                for k in range(KT):
                    kb.op("pe", lambda e: e.matmul(pa[:, :], w1s[s][:, k, :], hT[:, k, :], start=(k == 0), stop=(k == KT - 1)),
                          reads=[t_w1s[s], t_hT], writes=[ta], inc=(k == KT - 1))
                for k in range(KT):
                    kb.op("pe", lambda e: e.matmul(pb[:, :], w3s[s][:, k, :], hT[:, k, :], start=(k == 0), stop=(k == KT - 1)),
                          reads=[t_w3s[s], t_hT], writes=[tb], inc=(k == KT - 1))
                kb.op("act", lambda e: e.activation(sil[j % 2][:, :], pa[:, :], AF.Silu), reads=[ta], writes=[t_sil[j % 2]])
                kb.op("dve", lambda e: e.tensor_tensor(gate[:, j, :], sil[j % 2][:, :], pb[:, :], ALU.mult), reads=[t_sil[j % 2], tb], writes=[t_gate])
            for m in range(KT):
                s = m % 2
                kb.dma("pool", w2s[s][:, :, :], w2[m], writes=[t_w2s[s]])
                pc = env.ps[4 + s]; tc = env.t_ps[4 + s]
                for j in range(JT):
                    kb.op("pe", lambda e: e.matmul(pc[:, :], w2s[s][:, j, :], gate[:, j, :], start=(j == 0), stop=(j == JT - 1)),
                          reads=[t_w2s[s], t_gate], writes=[tc], inc=(j == JT - 1))
                kb.op("act", lambda e: e.copy(ys[:, m, :], pc[:, :]), reads=[tc], writes=[t_ys])
            rms_stats(kb, env, ys, t_ys, gate, t_gate, rstd2, t_rstd2, 7)
            for m in range(KT):
                s = m % 2
                kb.op("dve", lambda e: e.scalar_tensor_tensor(ot[s][:, :], ys[:, m, :], env.ghalf[:, gc_b + m:gc_b + m + 1], rstd2[:, :], ALU.mult, ALU.mult),
                      reads=[t_ys, env.t_ghalf, t_rstd2], writes=[t_ot[s]])
                kb.op("dve", lambda e: e.tensor_tensor(ot[s][:, :], ot[s][:, :], xs[:, m, :], ALU.add), reads=[t_ot[s], t_xs], writes=[t_ot[s]])
                kb.dma("sp", xTo[m * 128:(m + 1) * 128, c0:c0 + NB], ot[s][:, :], reads=[t_ot[s]])
        kb.barrier()
        kb.es = old


LW = 1024
PAD = 1024
NEG = -1e30


def evac(kb, eng, out, in_, reads, writes, func=None, scale=1.0):
    if eng == "act":
        kb.op("act", lambda e: e.activation(out, in_, func or AF.Copy, scale=scale), reads=reads, writes=writes)
    else:
        kb.op("dve", lambda e: e.tensor_copy(out, in_), reads=reads, writes=writes)


def norm_block(kb, env, xs, t_xs, hT, t_hT, sq, t_sq, rstd, t_rstd, gc):
    rms_stats(kb, env, xs, t_xs, sq, t_sq, rstd, t_rstd, 7)
    for k in range(KT):
        kb.op("dve", lambda e: e.scalar_tensor_tensor(hT[:, k, :], xs[:, k, :], env.gcol[:, gc + k:gc + k + 1], rstd[:, :], ALU.mult, ALU.mult),
              reads=[t_xs, env.t_gcol, t_rstd], writes=[t_hT])


def mixer_in_stage(kb, env, xTi, T, w_in, sc, gc):
    es = ExitStack()
    with es:
        old = kb.es; kb.es = es
        xs = kb.sb([128, KT, NB], F32); t_xs = Trk()
        hT = kb.sb([128, KT, NB], BF16); t_hT = Trk()
        sq = kb.sb([128, KT, NB], BF16); t_sq = Trk()
        rstd = kb.sb([128, NB], F32); t_rstd = Trk()
        ws = [kb.sb([128, KT, 128], BF16) for _ in range(3)]; t_ws = [Trk() for _ in range(3)]
        wv = [kb.sb([128, KT, 512], BF16) for _ in range(2)]; t_wv = [Trk() for _ in range(2)]
        st32 = [kb.sb([128, NB], F32) for _ in range(2)]; t_st32 = [Trk(), Trk()]
        st16 = [kb.sb([128, NB], BF16) for _ in range(3)]; t_st16 = [Trk() for _ in range(3)]
        n16 = 0
        for blk in range(T // NB):
            c0 = blk * NB
            kb.dma("sp", xs[:, :, :], xTi[:, c0:c0 + NB].rearrange("(k p) t -> p k t", p=128), writes=[t_xs])
            norm_block(kb, env, xs, t_xs, hT, t_hT, sq, t_sq, rstd, t_rstd, gc)
            it = 0
            for ct in list(range(0, 48)) + list(range(60, 108)):
                s = it % 3
                kb.dma("pool", ws[s][:, :, :], w_in[0][ct], writes=[t_ws[s]])
                pb = env.ps[it % 4]; tp = env.t_ps[it % 4]
                for k in range(KT):
                    kb.op("pe", lambda e: e.matmul(pb[:, :], ws[s][:, k, :], hT[:, k, :], start=(k == 0), stop=(k == KT - 1)),
                          reads=[t_ws[s], t_hT], writes=[tp], inc=(k == KT - 1))
                if ct < 8:
                    b = it % 2
                    evac(kb, "dve", st32[b][:, :], pb[:, :], [tp], [t_st32[b]])
                    kb.dma("sp", sc["zxl"][ct * 128:(ct + 1) * 128, 2 + c0:2 + c0 + NB], st32[b][:, :], reads=[t_st32[b]])
                else:
                    b = n16 % 3; n16 += 1
                    if ct < 16:
                        dst = sc["zgl"][(ct - 8) * 128:(ct - 7) * 128, c0:c0 + NB]; fn = AF.Gelu; scl = 1.0
                    elif ct < 24:
                        dst = sc["zu"][(ct - 16) * 128:(ct - 15) * 128, c0:c0 + NB]; fn = AF.Copy; scl = 1.0
                    elif ct < 36:
                        dst = sc["zq"][(ct - 24) * 128:(ct - 23) * 128, PAD + c0:PAD + c0 + NB]; fn = AF.Copy; scl = 0.125
                    elif ct < 48:
                        dst = sc["zk"][(ct - 36) * 128:(ct - 35) * 128, PAD + c0:PAD + c0 + NB]; fn = AF.Copy; scl = 1.0
                    else:
                        gi = (ct - 60) // 16
                        dst = sc["zg"][gi][((ct - 60) % 16) * 128:((ct - 60) % 16 + 1) * 128, c0:c0 + NB]; fn = AF.Sigmoid; scl = 1.0
                    evac(kb, "act", st16[b][:, :], pb[:, :], [tp], [t_st16[b]], func=fn, scale=scl)
                    kb.dma("sp", dst, st16[b][:, :], reads=[t_st16[b]])
                it += 1
            for vc in range(3):
                s = vc % 2
                kb.dma("pool", wv[s][:, :, :], w_in[1][vc], writes=[t_wv[s]])
                for tb in range(NB // 128):
                    pb = env.ps[4 + tb % 2]; tp = env.t_ps[4 + tb % 2]
                    for k in range(KT):
                        kb.op("pe", lambda e: e.matmul(pb[:, :], hT[:, k, tb * 128:(tb + 1) * 128], wv[s][:, k, :], start=(k == 0), stop=(k == KT - 1)),
                              reads=[t_wv[s], t_hT], writes=[tp], inc=(k == KT - 1))
                    b = n16 % 3; n16 += 1
                    evac(kb, "act", st16[b][:, :], pb[:, :], [tp], [t_st16[b]])
                    kb.dma("sp", sc["zv"][PAD + c0 + tb * 128:PAD + c0 + (tb + 1) * 128, vc * 512:(vc + 1) * 512], st16[b][:, :], reads=[t_st16[b]])
        kb.barrier()
        kb.es = old


def lru_prep(kb, env, cd, l):
    for nm in ("lru_L", "lru_ba", "lru_bx"):
        t = kb.sb([128, 16], F32, nm); tr = Trk()
        kb.dma("sp", t[:, :], cd[nm][:, l * 16:(l + 1) * 16], writes=[tr])
        setattr(env, nm, t); setattr(env, "t_" + nm, tr)
    env.lru_cw = kb.sb([128, 32], F32, "cw"); env.t_lru_cw = Trk()
    kb.dma("sp", env.lru_cw[:, :], cd["lru_cw"][:, l * 32:(l + 1) * 32], writes=[env.t_lru_cw])
    env.lru_cb = kb.sb([128, 8], F32, "cb"); env.t_lru_cb = Trk()
    kb.dma("sp", env.lru_cb[:, :], cd["lru_cb"][:, l * 8:(l + 1) * 8], writes=[env.t_lru_cb])
    env.lru_c = kb.sb([128, 16], F32, "lc"); env.t_lru_c = Trk()
    env.lru_c2 = kb.sb([128, 16], F32, "lc2"); env.t_lru_c2 = Trk()
    kb.op("act", lambda e: e.activation(env.lru_c[:, :], env.lru_L[:, :], AF.Exp, scale=-1.0), reads=[env.t_lru_L], writes=[env.t_lru_c])
    kb.op("act", lambda e: e.activation(env.lru_c[:, :], env.lru_c[:, :], AF.Ln, bias=env.one[:, 0:1], scale=1.0), reads=[env.t_lru_c, env.t_one], writes=[env.t_lru_c])
    kb.op("dve", lambda e: e.tensor_scalar(env.lru_c2[:, :], env.lru_c[:, :], -16.0, None, ALU.mult), reads=[env.t_lru_c], writes=[env.t_lru_c2])
    kb.op("dve", lambda e: e.tensor_scalar(env.lru_c[:, :], env.lru_c[:, :], -8.0, None, ALU.mult), reads=[env.t_lru_c], writes=[env.t_lru_c])
    env.lru_w = kb.sb([128, 32, 128], BF16, "lw"); env.t_lru_w = Trk()
    kb.dma("pool", env.lru_w[:, :, :], cd["lru_w"][l], writes=[env.t_lru_w])


def lru_stage(kb, env, T, sc):
    es = ExitStack()
    with es:
        old = kb.es; kb.es = es
        nb = T // NB
        xl = [kb.sb([128, NB + 4], F32) for _ in range(2)]; t_xl = [Trk(), Trk()]
        xc = kb.sb([128, NB], F32); t_xc = Trk()
        xcb = kb.sb([128, NB], BF16); t_xcb = Trk()
        r = kb.sb([128, NB], F32); t_r = Trk()
        ig = kb.sb([128, NB], F32); t_ig = Trk()
        a = kb.sb([128, NB], F32); t_a = Trk()
        bb = kb.sb([128, NB], F32); t_bb = Trk()
        h = [kb.sb([128, NB], F32) for _ in range(2)]; t_h = [Trk(), Trk()]
        hf = [kb.sb([128, NB], F32) for _ in range(2)]; t_hf = [Trk(), Trk()]
        gl = [kb.sb([128, NB], BF16) for _ in range(2)]; t_gl = [Trk(), Trk()]
        yo = [kb.sb([128, NB], BF16) for _ in range(2)]; t_yo = [Trk(), Trk()]
        st = kb.sb([128, 1], F32); t_st = Trk()
        it = 0
        for j in range(8):
            for d in range(2):
                kb.op("dve", lambda e: e.memset(st[:, :], 0.0), writes=[t_st])
                blks = range(nb) if d == 0 else range(nb - 1, -1, -1)
                for blk in blks:
                    c0 = blk * NB
                    s = it % 2; it += 1
                    kb.dma("sp", xl[s][:, 0:NB + 3], sc["zxl"][j * 128:(j + 1) * 128, c0:c0 + NB + 3], writes=[t_xl[s]])
                    cw = lambda kk: env.lru_cw[:, kk * 8 + j:kk * 8 + j + 1]
                    kb.op("dve", lambda e: e.tensor_scalar(xc[:, :], xl[s][:, 0:NB], cw(0), env.lru_cb[:, j:j + 1], ALU.mult, ALU.add),
                          reads=[t_xl[s], env.t_lru_cw, env.t_lru_cb], writes=[t_xc])
                    for kk in range(1, 4):
                        kb.op("dve", lambda e: e.scalar_tensor_tensor(xc[:, :], xl[s][:, kk:kk + NB], cw(kk), xc[:, :], ALU.mult, ALU.add),
                              reads=[t_xl[s], env.t_lru_cw, t_xc], writes=[t_xc])
                    kb.op("act", lambda e: e.copy(xcb[:, :], xc[:, :]), reads=[t_xc], writes=[t_xcb])
                    pa = env.ps[0 + 2 * (it % 2)]; ta = env.t_ps[0 + 2 * (it % 2)]
                    pb = env.ps[1 + 2 * (it % 2)]; tb = env.t_ps[1 + 2 * (it % 2)]
                    wi = (d * 8 + j) * 2
                    kb.op("pe", lambda e: e.matmul(pa[:, :], env.lru_w[:, wi, :], xcb[:, :], start=True, stop=True), reads=[env.t_lru_w, t_xcb], writes=[ta])
                    kb.op("pe", lambda e: e.matmul(pb[:, :], env.lru_w[:, wi + 1, :], xcb[:, :], start=True, stop=True), reads=[env.t_lru_w, t_xcb], writes=[tb])
                    col = d * 8 + j
                    kb.op("act", lambda e: e.activation(r[:, :], pa[:, :], AF.Sigmoid, bias=env.lru_ba[:, col:col + 1]), reads=[ta, env.t_lru_ba], writes=[t_r])
                    kb.op("act", lambda e: e.activation(ig[:, :], pb[:, :], AF.Sigmoid, bias=env.lru_bx[:, col:col + 1]), reads=[tb, env.t_lru_bx], writes=[t_ig])
                    kb.op("act", lambda e: e.activation(a[:, :], r[:, :], AF.Exp, scale=env.lru_c[:, col:col + 1]), reads=[t_r, env.t_lru_c], writes=[t_a])
                    kb.op("act", lambda e: e.activation(r[:, :], r[:, :], AF.Exp, scale=env.lru_c2[:, col:col + 1]), reads=[t_r, env.t_lru_c2], writes=[t_r])
                    kb.op("dve", lambda e: e.tensor_scalar(r[:, :], r[:, :], 1.0, None, ALU.min), reads=[t_r], writes=[t_r])
                    kb.op("act", lambda e: e.activation(r[:, :], r[:, :], AF.Sqrt, bias=env.one[:, 0:1], scale=-1.0), reads=[t_r, env.t_one], writes=[t_r])
                    kb.op("dve", lambda e: e.tensor_tensor(bb[:, :], ig[:, :], xc[:, :], ALU.mult), reads=[t_ig, t_xc], writes=[t_bb])
                    kb.op("dve", lambda e: e.tensor_tensor(bb[:, :], bb[:, :], r[:, :], ALU.mult), reads=[t_bb, t_r], writes=[t_bb])
                    if d == 0:
                        kb.op("dve", lambda e: e.tensor_tensor_scan(h[s][:, :], a[:, :], bb[:, :], st[:, 0:1], ALU.mult, ALU.add), reads=[t_a, t_bb, t_st], writes=[t_h[s]])
                        kb.op("dve", lambda e: e.tensor_copy(st[:, :], h[s][:, NB - 1:NB]), reads=[t_h[s]], writes=[t_st])
                        kb.dma("sp", sc["hf"][j * 128:(j + 1) * 128, c0:c0 + NB], h[s][:, :], reads=[t_h[s]], writes=[sc["t_hf"]])
                    else:
                        kb.dma("sp", hf[s][:, :], sc["hf"][j * 128:(j + 1) * 128, c0:c0 + NB], reads=[sc["t_hf"]], writes=[t_hf[s]])
                        kb.dma("sp", gl[s][:, :], sc["zgl"][j * 128:(j + 1) * 128, c0:c0 + NB], writes=[t_gl[s]])
                        kb.op("dve", lambda e: e.tensor_tensor_scan(h[s][:, ::-1], a[:, ::-1], bb[:, ::-1], st[:, 0:1], ALU.mult, ALU.add), reads=[t_a, t_bb, t_st], writes=[t_h[s]])
                        kb.op("dve", lambda e: e.tensor_copy(st[:, :], h[s][:, 0:1]), reads=[t_h[s]], writes=[t_st])
                        kb.op("dve", lambda e: e.tensor_tensor(h[s][:, :], h[s][:, :], hf[s][:, :], ALU.add), reads=[t_h[s], t_hf[s]], writes=[t_h[s]])
                        kb.op("dve", lambda e: e.tensor_tensor(yo[s][:, :], h[s][:, :], gl[s][:, :], ALU.mult), reads=[t_h[s], t_gl[s]], writes=[t_yo[s]])
                        kb.dma("sp", sc["ylru"][j * 128:(j + 1) * 128, c0:c0 + NB], yo[s][:, :], reads=[t_yo[s]])
        kb.barrier()
        kb.es = old

import math
TWO_PI = 2.0 * math.pi


def frac_sin(kb, out, t_out, ph, t_ph, tmpi, t_tmpi, tmpf, t_tmpf, shift):
    src = ph
    if shift != 0.0:
        kb.op("dve", lambda e: e.tensor_scalar(tmpf, ph, shift, None, ALU.add), reads=[t_ph], writes=[t_tmpf])
        kb.op("dve", lambda e: e.tensor_copy(tmpi, tmpf), reads=[t_tmpf], writes=[t_tmpi])
        kb.op("dve", lambda e: e.tensor_copy(out, tmpi), reads=[t_tmpi], writes=[t_out])
        kb.op("dve", lambda e: e.tensor_tensor(out, tmpf, out, ALU.subtract), reads=[t_tmpf, t_out], writes=[t_out])
    else:
        kb.op("dve", lambda e: e.tensor_copy(tmpi, ph), reads=[t_ph], writes=[t_tmpi])
        kb.op("dve", lambda e: e.tensor_copy(out, tmpi), reads=[t_tmpi], writes=[t_out])
        kb.op("dve", lambda e: e.tensor_tensor(out, ph, out, ALU.subtract), reads=[t_ph, t_out], writes=[t_out])
    kb.op("act", lambda e: e.activation(out, out, AF.Sin, scale=TWO_PI), reads=[t_out], writes=[t_out])


def s5_prep(kb, env, cd, l, sc):
    es = ExitStack()
    with es:
        old = kb.es; kb.es = es
        def ld(nm):
            t = kb.sb([128, 64], F32); tr = Trk()
            kb.dma("sp", t[:, :], cd[nm][:, l * 64:(l + 1) * 64], writes=[tr])
            return t, tr
        lr, t_lr = ld("s5_lr"); li, t_li = ld("s5_li"); dt, t_dt = ld("s5_ldt")
        mk = lambda: (kb.sb([128, 64], F32), Trk())
        f, t_f = mk(); sn, t_sn = mk(); cs, t_cs = mk(); ar, t_ar = mk(); ai, t_ai = mk(); den, t_den = mk()
        tf, t_tf = mk(); q1, t_q1 = mk(); q2, t_q2 = mk()
        ti = kb.sb([128, 64], I32); t_ti = Trk()
        kb.op("act", lambda e: e.activation(dt[:, :], dt[:, :], AF.Exp), reads=[t_dt], writes=[t_dt])
        kb.op("dve", lambda e: e.tensor_tensor(f[:, :], lr[:, :], dt[:, :], ALU.mult), reads=[t_lr, t_dt], writes=[t_f])
        kb.op("act", lambda e: e.activation(env.s5_rho[:, :], f[:, :], AF.Exp), reads=[t_f], writes=[env.t_s5_rho])
        kb.op("dve", lambda e: e.tensor_tensor(f[:, :], li[:, :], dt[:, :], ALU.mult), reads=[t_li, t_dt], writes=[t_f])
        kb.op("dve", lambda e: e.tensor_scalar(env.s5_f[:, :], f[:, :], 1.0 / TWO_PI, None, ALU.mult), reads=[t_f], writes=[env.t_s5_f])
        frac_sin(kb, sn[:, :], t_sn, env.s5_f[:, :], env.t_s5_f, ti[:, :], t_ti, tf[:, :], t_tf, 0.0)
        frac_sin(kb, cs[:, :], t_cs, env.s5_f[:, :], env.t_s5_f, ti[:, :], t_ti, tf[:, :], t_tf, 0.25)
        kb.op("dve", lambda e: e.tensor_tensor(ar[:, :], env.s5_rho[:, :], cs[:, :], ALU.mult), reads=[env.t_s5_rho, t_cs], writes=[t_ar])
        kb.op("dve", lambda e: e.tensor_tensor(ai[:, :], env.s5_rho[:, :], sn[:, :], ALU.mult), reads=[env.t_s5_rho, t_sn], writes=[t_ai])
        kb.op("dve", lambda e: e.tensor_scalar(ar[:, :], ar[:, :], -1.0, None, ALU.add), reads=[t_ar], writes=[t_ar])
        kb.op("dve", lambda e: e.tensor_tensor(den[:, :], lr[:, :], lr[:, :], ALU.mult), reads=[t_lr], writes=[t_den])
        kb.op("dve", lambda e: e.tensor_tensor(q1[:, :], li[:, :], li[:, :], ALU.mult), reads=[t_li], writes=[t_q1])
        kb.op("dve", lambda e: e.tensor_tensor(den[:, :], den[:, :], q1[:, :], ALU.add), reads=[t_den, t_q1], writes=[t_den])
        kb.op("dve", lambda e: e.reciprocal(den[:, :], den[:, :]), reads=[t_den], writes=[t_den])
        kb.op("dve", lambda e: e.tensor_tensor(q1[:, :], ar[:, :], lr[:, :], ALU.mult), reads=[t_ar, t_lr], writes=[t_q1])
        kb.op("dve", lambda e: e.tensor_tensor(q2[:, :], ai[:, :], li[:, :], ALU.mult), reads=[t_ai, t_li], writes=[t_q2])
        kb.op("dve", lambda e: e.tensor_tensor(q1[:, :], q1[:, :], q2[:, :], ALU.add), reads=[t_q1, t_q2], writes=[t_q1])
        kb.op("dve", lambda e: e.tensor_tensor(env.s5_cr[:, :], q1[:, :], den[:, :], ALU.mult), reads=[t_q1, t_den], writes=[env.t_s5_cr])
        kb.op("dve", lambda e: e.tensor_tensor(q1[:, :], ai[:, :], lr[:, :], ALU.mult), reads=[t_ai, t_lr], writes=[t_q1])
        kb.op("dve", lambda e: e.tensor_tensor(q2[:, :], ar[:, :], li[:, :], ALU.mult), reads=[t_ar, t_li], writes=[t_q2])
        kb.op("dve", lambda e: e.tensor_tensor(q1[:, :], q1[:, :], q2[:, :], ALU.subtract), reads=[t_q1, t_q2], writes=[t_q1])
        kb.op("dve", lambda e: e.tensor_tensor(env.s5_ci[:, :], q1[:, :], den[:, :], ALU.mult), reads=[t_q1, t_den], writes=[env.t_s5_ci])
        kb.op("dve", lambda e: e.tensor_scalar(env.s5_ncr[:, :], env.s5_cr[:, :], -1.0, None, ALU.mult), reads=[env.t_s5_cr], writes=[env.t_s5_ncr])
        kb.op("dve", lambda e: e.tensor_scalar(env.s5_nci[:, :], env.s5_ci[:, :], -1.0, None, ALU.mult), reads=[env.t_s5_ci], writes=[env.t_s5_nci])
        kb.dma("pool", env.s5_Bw[:, :, :], cd["s5_B"][l], writes=[env.t_s5_Bw])
        NT = NB + 4
        ctr = [kb.sb([128, 128], F32) for _ in range(2)]; t_ctr = [Trk(), Trk()]
        cti = [kb.sb([128, 128], F32) for _ in range(2)]; t_cti = [Trk(), Trk()]
        tq = kb.sb([128, 128], F32); t_tq = Trk()
        ph = kb.sb([128, NT], F32); t_ph = Trk()
        tbi = kb.sb([128, NT], I32); t_tbi = Trk()
        tbf = kb.sb([128, NT], F32); t_tbf = Trk()
        tb = [kb.sb([128, 2, NT], F32) for _ in range(2)]; t_tb = [Trk(), Trk()]
        for col in range(64):
            s = col % 2
            kb.dma("sp", ctr[s][:, :], cd["s5_C"][l, col, 0], writes=[t_ctr[s]])
            kb.dma("sp", cti[s][:, :], cd["s5_C"][l, col, 1], writes=[t_cti[s]])
            kb.op("dve", lambda e: e.tensor_scalar(tq[:, :], ctr[s][:, :], env.s5_cr[:, col:col + 1], None, ALU.mult), reads=[t_ctr[s], env.t_s5_cr], writes=[t_tq])
            kb.op("dve", lambda e: e.scalar_tensor_tensor(env.s5_Cw[:, col * 2, :], cti[s][:, :], env.s5_nci[:, col:col + 1], tq[:, :], ALU.mult, ALU.add),
                  reads=[t_cti[s], env.t_s5_nci, t_tq], writes=[env.t_s5_Cw])
            kb.op("dve", lambda e: e.tensor_scalar(tq[:, :], ctr[s][:, :], env.s5_nci[:, col:col + 1], None, ALU.mult), reads=[t_ctr[s], env.t_s5_nci], writes=[t_tq])
            kb.op("dve", lambda e: e.scalar_tensor_tensor(env.s5_Cw[:, col * 2 + 1, :], cti[s][:, :], env.s5_ncr[:, col:col + 1], tq[:, :], ALU.mult, ALU.add),
                  reads=[t_cti[s], env.t_s5_ncr, t_tq], writes=[env.t_s5_Cw])
            kb.op("dve", lambda e: e.tensor_scalar(ph[:, :], env.iota[:, 0:NT], env.s5_f[:, col:col + 1], None, ALU.mult), reads=[env.t_iota, env.t_s5_f], writes=[t_ph])
            frac_sin(kb, tb[s][:, 1, :], t_tb[s], ph[:, :], t_ph, tbi[:, :], t_tbi, tbf[:, :], t_tbf, 0.0)
            frac_sin(kb, tb[s][:, 0, :], t_tb[s], ph[:, :], t_ph, tbi[:, :], t_tbi, tbf[:, :], t_tbf, 0.25)
            kb.dma("sp", sc["tab"][col].rearrange("a p n -> p a n"), tb[s][:, :, :], reads=[t_tb[s]])
        kb.barrier()
        kb.es = old


def s5_stage(kb, env, T, sc, l):
    es = ExitStack()
    with es:
        old = kb.es; kb.es = es
        nb = T // NB
        NT = NB + 4
        F = lambda: (kb.sb([128, NB], F32), Trk())
        u = [kb.sb([128, NB], BF16) for _ in range(2)]; t_u = [Trk(), Trk()]
        tb = [kb.sb([128, 2, NT], F32) for _ in range(2)]; t_tb = [Trk(), Trk()]
        t1, t_t1 = F(); t2, t_t2 = F(); t3, t_t3 = F(); t4, t_t4 = F(); mr, t_mr = F(); mi, t_mi = F(); wr, t_wr = F(); wi, t_wi = F()
        xr = kb.sb([128, NB], BF16); t_xr = Trk(); xi = kb.sb([128, NB], BF16); t_xi = Trk()
        rho = kb.sb([128, NB], F32); t_rho = Trk()
        st = kb.sb([128, 64, 2], F32); t_st = Trk()
        ini = kb.sb([128, 2], F32); t_ini = Trk(); tt = kb.sb([128, 2], F32); t_tt = Trk()
        yst = [kb.sb([128, NB], F32) for _ in range(2)]; t_yst = [Trk(), Trk()]
        yfl = [kb.sb([128, NB], F32) for _ in range(2)]; t_yfl = [Trk(), Trk()]
        y1 = [kb.sb([128, NB], BF16) for _ in range(2)]; t_y1 = [Trk(), Trk()]
        kb.op("dve", lambda e: e.memset(st[:, :, :], 0.0), writes=[t_st])
        it = 0; itb = 0
        for d in range(2):
            for c in range(8):
                blks = range(nb) if d == 0 else range(nb - 1, -1, -1)
                for blk in blks:
                    c0 = blk * NB
                    s = it % 2; it += 1
                    kb.dma("sp", u[s][:, :], sc["zu"][c * 128:(c + 1) * 128, c0:c0 + NB], writes=[t_u[s]])
                    py = env.ps[4 + s]; tpy = env.t_ps[4 + s]
                    for ii in range(4):
                        col = d * 32 + c * 4 + ii
                        sb_ = itb % 2; itb += 1
                        kb.dma("sp", tb[sb_][:, :, :], sc["tab"][col].rearrange("a p n -> p a n"), writes=[t_tb[sb_]])
                        cs_ = tb[sb_][:, 0, 0:NB]; sn_ = tb[sb_][:, 1, 0:NB]
                        rv = (lambda ap: ap) if d == 0 else (lambda ap: ap[:, ::-1])
                        cs_ = rv(cs_); sn_ = rv(sn_)
                        pr = env.ps[0 + 2 * (itb % 2)]; tpr = env.t_ps[0 + 2 * (itb % 2)]
                        pi = env.ps[1 + 2 * (itb % 2)]; tpi = env.t_ps[1 + 2 * (itb % 2)]
                        kb.op("pe", lambda e: e.matmul(pr[:, :], env.s5_Bw[:, col * 2, :], u[s][:, :], start=True, stop=True), reads=[env.t_s5_Bw, t_u[s]], writes=[tpr])
                        kb.op("pe", lambda e: e.matmul(pi[:, :], env.s5_Bw[:, col * 2 + 1, :], u[s][:, :], start=True, stop=True), reads=[env.t_s5_Bw, t_u[s]], writes=[tpi])
                        kb.op("dve", lambda e: e.tensor_tensor(t1[:, :], pr[:, :], cs_, ALU.mult), reads=[tpr, t_tb[sb_]], writes=[t_t1])
                        kb.op("dve", lambda e: e.tensor_tensor(t2[:, :], pi[:, :], sn_, ALU.mult), reads=[tpi, t_tb[sb_]], writes=[t_t2])
                        kb.op("dve", lambda e: e.tensor_tensor(mr[:, :], t1[:, :], t2[:, :], ALU.add), reads=[t_t1, t_t2], writes=[t_mr])
                        kb.op("dve", lambda e: e.tensor_tensor(t1[:, :], pi[:, :], cs_, ALU.mult), reads=[tpi, t_tb[sb_]], writes=[t_t1])
                        kb.op("dve", lambda e: e.tensor_tensor(t2[:, :], pr[:, :], sn_, ALU.mult), reads=[tpr, t_tb[sb_]], writes=[t_t2])
                        kb.op("dve", lambda e: e.tensor_tensor(mi[:, :], t1[:, :], t2[:, :], ALU.subtract), reads=[t_t1, t_t2], writes=[t_mi])
                        cL = tb[sb_][:, 0, NB:NB + 1]; sL = tb[sb_][:, 1, NB:NB + 1]
                        kb.op("dve", lambda e: e.tensor_scalar(tt[:, 0:1], st[:, col, 1:2], sL, None, ALU.mult), reads=[t_st, t_tb[sb_]], writes=[t_tt])
                        kb.op("dve", lambda e: e.scalar_tensor_tensor(ini[:, 0:1], st[:, col, 0:1], cL, tt[:, 0:1], ALU.mult, ALU.subtract), reads=[t_st, t_tb[sb_], t_tt], writes=[t_ini])
                        kb.op("dve", lambda e: e.tensor_scalar(tt[:, 1:2], st[:, col, 1:2], cL, None, ALU.mult), reads=[t_st, t_tb[sb_]], writes=[t_tt])
                        kb.op("dve", lambda e: e.scalar_tensor_tensor(ini[:, 1:2], st[:, col, 0:1], sL, tt[:, 1:2], ALU.mult, ALU.add), reads=[t_st, t_tb[sb_], t_tt], writes=[t_ini])
                        rho_b = env.s5_rho[:, col:col + 1].to_broadcast([128, NB])
                        kb.op("dve", lambda e: e.tensor_tensor_scan(rv(wr[:, :]), rho_b, rv(mr[:, :]), ini[:, 0:1], ALU.mult, ALU.add), reads=[env.t_s5_rho, t_mr, t_ini], writes=[t_wr])
                        kb.op("dve", lambda e: e.tensor_tensor_scan(rv(wi[:, :]), rho_b, rv(mi[:, :]), ini[:, 1:2], ALU.mult, ALU.add), reads=[env.t_s5_rho, t_mi, t_ini], writes=[t_wi])
                        e0 = NB - 1 if d == 0 else 0
                        kb.op("dve", lambda e: e.tensor_copy(st[:, col, 0:1], wr[:, e0:e0 + 1]), reads=[t_wr], writes=[t_st])
                        kb.op("dve", lambda e: e.tensor_copy(st[:, col, 1:2], wi[:, e0:e0 + 1]), reads=[t_wi], writes=[t_st])
                        kb.op("dve", lambda e: e.tensor_tensor(t3[:, :], wr[:, :], cs_, ALU.mult), reads=[t_wr, t_tb[sb_]], writes=[t_t3])
                        kb.op("dve", lambda e: e.tensor_tensor(t4[:, :], wi[:, :], sn_, ALU.mult), reads=[t_wi, t_tb[sb_]], writes=[t_t4])
                        kb.op("dve", lambda e: e.tensor_tensor(xr[:, :], t3[:, :], t4[:, :], ALU.subtract), reads=[t_t3, t_t4], writes=[t_xr])
                        kb.op("dve", lambda e: e.tensor_tensor(t3[:, :], wr[:, :], sn_, ALU.mult), reads=[t_wr, t_tb[sb_]], writes=[t_t3])
                        kb.op("dve", lambda e: e.tensor_tensor(t4[:, :], wi[:, :], cs_, ALU.mult), reads=[t_wi, t_tb[sb_]], writes=[t_t4])
                        kb.op("dve", lambda e: e.tensor_tensor(xi[:, :], t3[:, :], t4[:, :], ALU.add), reads=[t_t3, t_t4], writes=[t_xi])
                        kb.op("pe", lambda e: e.matmul(py[:, :], env.s5_Cw[:, col * 2, :], xr[:, :], start=(ii == 0), stop=False), reads=[env.t_s5_Cw, t_xr], writes=[tpy], inc=False)
                        kb.op("pe", lambda e: e.matmul(py[:, :], env.s5_Cw[:, col * 2 + 1, :], xi[:, :], start=False, stop=(ii == 3)), reads=[env.t_s5_Cw, t_xi], writes=[tpy], inc=True)
                    if d == 0:
                        kb.op("act", lambda e: e.copy(yst[s][:, :], py[:, :]), reads=[tpy], writes=[t_yst[s]])
                        kb.dma("sp", sc["yf"][c * 128:(c + 1) * 128, c0:c0 + NB], yst[s][:, :], reads=[t_yst[s]], writes=[sc["t_yf"]])
                    else:
                        kb.dma("sp", yfl[s][:, :], sc["yf"][c * 128:(c + 1) * 128, c0:c0 + NB], reads=[sc["t_yf"]], writes=[t_yfl[s]])
                        kb.op("dve", lambda e: e.tensor_tensor(yst[s][:, :], py[:, :], yfl[s][:, :], ALU.add), reads=[tpy, t_yfl[s]], writes=[t_yst[s]])
                        kb.op("dve", lambda e: e.scalar_tensor_tensor(yst[s][:, :], u[s][:, :], env.s5_d[:, l * 8 + c:l * 8 + c + 1], yst[s][:, :], ALU.mult, ALU.add),
                              reads=[t_u[s], env.t_s5_d, t_yst[s]], writes=[t_yst[s]])
                        kb.op("act", lambda e: e.activation(y1[s][:, :], yst[s][:, :], AF.Gelu), reads=[t_yst[s]], writes=[t_y1[s]])
                        kb.dma("sp", sc["y1"][c * 128:(c + 1) * 128, c0:c0 + NB], y1[s][:, :], reads=[t_y1[s]])
        kb.barrier()
        kb.es = old


def glu_stage(kb, env, T, sc, glu_w, l):
    es = ExitStack()
    with es:
        old = kb.es; kb.es = es
        gw = kb.sb([128, 8, LW], BF16); t_gw = Trk()
        kb.dma("pool", gw[:, :, :], glu_w.rearrange("(k p) c -> p k c", p=128), writes=[t_gw])
        y1 = [kb.sb([128, 8, NB], BF16) for _ in range(2)]; t_y1 = [Trk(), Trk()]
        sg = [kb.sb([128, NB], F32) for _ in range(2)]; t_sg = [Trk(), Trk()]
        yo = [kb.sb([128, NB], BF16) for _ in range(2)]; t_yo = [Trk(), Trk()]
        it = 0
        for blk in range(T // NB):
            c0 = blk * NB
            sb_ = blk % 2
            kb.dma("sp", y1[sb_][:, :, :], sc["y1"][:, c0:c0 + NB].rearrange("(k p) t -> p k t", p=128), writes=[t_y1[sb_]])
            for m in range(8):
                s = it % 2; it += 1
                pb = env.ps[s]; tp = env.t_ps[s]
                for k in range(8):
                    kb.op("pe", lambda e: e.matmul(pb[:, :], gw[:, k, m * 128:(m + 1) * 128], y1[sb_][:, k, :], start=(k == 0), stop=(k == 7)),
                          reads=[t_gw, t_y1[sb_]], writes=[tp], inc=(k == 7))
                kb.op("act", lambda e: e.activation(sg[s][:, :], pb[:, :], AF.Sigmoid, bias=env.glu_b[:, l * 8 + m:l * 8 + m + 1]), reads=[tp, env.t_glu_b], writes=[t_sg[s]])
                kb.op("dve", lambda e: e.tensor_tensor(yo[s][:, :], sg[s][:, :], y1[sb_][:, m, :], ALU.mult), reads=[t_sg[s], t_y1[sb_]], writes=[t_yo[s]])
                kb.dma("sp", sc["ys5"][m * 128:(m + 1) * 128, c0:c0 + NB], yo[s][:, :], reads=[t_yo[s]])
        kb.barrier()
        kb.es = old


def merge_stage(kb, env, xTi, xTo, T, sc, wbl, wbs, wba, wout, gc):
    es = ExitStack()
    with es:
        old = kb.es; kb.es = es
        xs = kb.sb([128, KT, NB], F32); t_xs = Trk()
        ys = kb.sb([128, KT, NB], F32); t_ys = Trk()
        sq = kb.sb([128, KT, NB], BF16); t_sq = Trk()
        mT = kb.sb([128, KT, NB], BF16); t_mT = Trk()
        yin = kb.sb([128, 20, NB], BF16); t_yin = Trk()
        gts = [kb.sb([128, 3, NB], BF16) for _ in range(2)]; t_gts = [Trk(), Trk()]
        wb = [kb.sb([128, 20, 128], BF16) for _ in range(2)]; t_wb = [Trk(), Trk()]
        wo = [kb.sb([128, KT, 128], BF16) for _ in range(2)]; t_wo = [Trk(), Trk()]
        acc = kb.sb([128, NB], F32); t_acc = Trk(); tm = kb.sb([128, NB], F32); t_tm = Trk()
        rstd2 = kb.sb([128, NB], F32); t_rstd2 = Trk()
        ot = [kb.sb([128, NB], F32) for _ in range(2)]; t_ot = [Trk(), Trk()]
        for blk in range(T // NB):
            c0 = blk * NB
            kb.dma("sp", xs[:, :, :], xTi[:, c0:c0 + NB].rearrange("(k p) t -> p k t", p=128), writes=[t_xs])
            kb.dma("sp", yin[:, 0:8, :], sc["ylru"][:, c0:c0 + NB].rearrange("(k p) t -> p k t", p=128), writes=[t_yin])
            kb.dma("sp", yin[:, 8:16, :], sc["ys5"][:, c0:c0 + NB].rearrange("(k p) t -> p k t", p=128), writes=[t_yin])
            kb.dma("sp", yin[:, 16:20, :], sc["yatt"][:, c0:c0 + NB].rearrange("(k p) t -> p k t", p=128), writes=[t_yin])
            for m in range(KT):
                s = m % 2
                kb.dma("pool", wb[s][:, :, :], wbl[m], writes=[t_wb[s]])
                for gi in range(3):
                    kb.dma("sp", gts[s][:, gi, :], sc["zg"][gi][m * 128:(m + 1) * 128, c0:c0 + NB], writes=[t_gts[s]])
                rng_ = [(0, 8), (8, 16), (16, 20)]
                for gi in range(3):
                    pb = env.ps[gi + 3 * s]; tp = env.t_ps[gi + 3 * s]
                    lo, hi = rng_[gi]
                    for k in range(lo, hi):
                        kb.op("pe", lambda e: e.matmul(pb[:, :], wb[s][:, k, :], yin[:, k, :], start=(k == lo), stop=(k == hi - 1)),
                              reads=[t_wb[s], t_yin], writes=[tp], inc=(k == hi - 1))
                    if gi == 0:
                        kb.op("dve", lambda e: e.tensor_tensor(acc[:, :], pb[:, :], gts[s][:, gi, :], ALU.mult), reads=[tp, t_gts[s]], writes=[t_acc])
                    else:
                        kb.op("dve", lambda e: e.tensor_tensor(tm[:, :], pb[:, :], gts[s][:, gi, :], ALU.mult), reads=[tp, t_gts[s]], writes=[t_tm])
                        if gi == 1:
                            kb.op("dve", lambda e: e.tensor_tensor(acc[:, :], acc[:, :], tm[:, :], ALU.add), reads=[t_acc, t_tm], writes=[t_acc])
                        else:
                            kb.op("dve", lambda e: e.tensor_tensor(mT[:, m, :], acc[:, :], tm[:, :], ALU.add), reads=[t_acc, t_tm], writes=[t_mT])
            for n in range(KT):
                s = n % 2
                kb.dma("pool", wo[s][:, :, :], wout[n], writes=[t_wo[s]])
                pc = env.ps[6 + s]; tc = env.t_ps[6 + s]
                for k in range(KT):
                    kb.op("pe", lambda e: e.matmul(pc[:, :], wo[s][:, k, :], mT[:, k, :], start=(k == 0), stop=(k == KT - 1)),
                          reads=[t_wo[s], t_mT], writes=[tc], inc=(k == KT - 1))
                kb.op("act", lambda e: e.copy(ys[:, n, :], pc[:, :]), reads=[tc], writes=[t_ys])
            rms_stats(kb, env, ys, t_ys, sq, t_sq, rstd2, t_rstd2, 5)
            for m in range(KT):
                s = m % 2
                kb.op("dve", lambda e: e.scalar_tensor_tensor(ot[s][:, :], ys[:, m, :], env.gcol[:, gc + m:gc + m + 1], rstd2[:, :], ALU.mult, ALU.mult),
                      reads=[t_ys, env.t_gcol, t_rstd2], writes=[t_ot[s]])
                kb.op("dve", lambda e: e.tensor_tensor(ot[s][:, :], ot[s][:, :], xs[:, m, :], ALU.add), reads=[t_ot[s], t_xs], writes=[t_ot[s]])
                kb.dma("sp", xTo[m * 128:(m + 1) * 128, c0:c0 + NB], ot[s][:, :], reads=[t_ot[s]])
        kb.barrier()
        kb.es = old


DIL = (1, 4, 16)
SI = lambda s: (s % 2) * 4 + s // 2
PC = 2048


def attn_stage(kb, env, T, sc, cd):
    nc = kb.nc
    es = ExitStack()
    with es:
        old = kb.es; kb.es = es
        relb = kb.sb([32, 24], F32); t_relb = Trk()
        kb.dma("sp", relb[:, :], cd["rel_bias"][:, :], writes=[t_relb])
        oh = kb.sb([32, 3 * 384], F32); t_oh = Trk()
        kb.dma("sp", oh[:, :], cd["oh"][:, :], writes=[t_oh])
        negm = kb.sb([8, 384], F32); t_negm = Trk()
        kb.dma("sp", negm[:, :], cd["negm"][:, :], writes=[t_negm])
        gsb = kb.sb([8, 384], F32); t_gsb = Trk()
        for g in range(3):
            pb = env.ps[g]; tp = env.t_ps[g]
            kb.op("pe", lambda e: e.matmul(pb[0:8, 0:384], relb[:, g * 8:(g + 1) * 8], oh[:, g * 384:(g + 1) * 384], start=True, stop=True), reads=[t_relb, t_oh], writes=[tp])
            kb.op("dve", lambda e: e.tensor_tensor(gsb[:, :], pb[0:8, 0:384], negm[:, :], ALU.add), reads=[tp, t_negm], writes=[t_gsb])
            kb.dma("sp", sc["Gd"][g * 8:(g + 1) * 8, :], gsb[:, :], reads=[t_gsb], writes=[sc["t_Gd"]])
        biasT = kb.sb([128, 3, 2, 8, 128], F32); t_biasT = Trk()
        gd_t = sc["Gd"].tensor if hasattr(sc["Gd"], "tensor") else sc["Gd"]
        antiI = kb.sb([128, 128], F32); t_antiI = Trk()
        kb.dma("sp", antiI[:, :], cd["antiI"][:, :], writes=[t_antiI])
        hank = [kb.sb([128, 8, 128], F32) for _ in range(2)]; t_hank = [Trk(), Trk()]
        psB = env.psbig[:, 0:2, :].rearrange("p a b -> p (a b)"); t_psB = Trk()
        for g in range(3):
            for o in range(2):
                hk = hank[o]; thk = t_hank[o]
                src = bass.AP(tensor=gd_t, offset=g * 8 * 384 + 128 - 128 * o, ap=[[1, 128], [384, 8], [1, 128]])
                kb.dma("sp", hk[:, :, :], src, reads=[sc["t_Gd"]], writes=[thk])
                for s in range(8):
                    kb.op("pe", lambda e: e.matmul(psB[:, SI(s) * 128:(SI(s) + 1) * 128], antiI[:, :], hk[:, s, :], start=True, stop=True), reads=[t_antiI, thk], writes=[t_psB], inc=(s == 7))
                kb.op("dve", lambda e: e.tensor_copy(biasT[:, g, o, :, :].rearrange("p s q -> p (s q)"), psB), reads=[t_psB], writes=[t_biasT])
        onesp = kb.sb([128, 2, 128], BF16); t_onesp = Trk()
        kb.op("dve", lambda e: e.memset(onesp[:, :, :], 0.0), writes=[t_onesp])
        kb.op("dve", lambda e: e.memset(onesp[:, 0, 0:64], 1.0), writes=[t_onesp])
        kb.op("dve", lambda e: e.memset(onesp[:, 1, 64:128], 1.0), writes=[t_onesp])
        vpad = [kb.sb([128, 8, 128], BF16) for _ in range(4)]; t_vpad = [Trk() for _ in range(4)]
        for v in vpad:
            kb.op("dve", lambda e: e.memset(v[:, :, :], 0.0), writes=t_vpad)
        accO = kb.sb([128, 4, PC], F32); t_accO = Trk()
        accZ = kb.sb([128, 4, PC], F32); t_accZ = Trk()
        kch = kb.sb([128, 4, PC + 2048], BF16); t_kch = Trk()
        qch = kb.sb([128, 4, PC], BF16); t_qch = Trk()
        sbt = [kb.sb([128, 1024], F32) for _ in range(2)]; t_sbt = [Trk(), Trk()]
        pT = [kb.sb([128, 8, 128], BF16) for _ in range(2)]; t_pT = [Trk(), Trk()]
        yo = kb.sb([128, 4, PC], BF16); t_yo = Trk()
        psS = [env.psbig[:, 0:2, :].rearrange("p a b -> p (a b)"), env.psbig[:, 2:4, :].rearrange("p a b -> p (a b)")]
        t_psS = [Trk(), Trk()]
        psO = env.ps[4]; t_psO = Trk(); psZ = env.ps[5]; t_psZ = Trk()
        zv = sc["zv"]
        vi = 0
        import os
        for ch in range(0 if os.environ.get('ATT_CUT') == '1' else T // PC):
            p0 = ch * PC
            for g in range(1 if os.environ.get('ATT_G0') == '1' else 3):
                d = DIL[g]
                kb.dma("sp", kch[:, :, :], sc["zk"][g * 512:(g + 1) * 512, p0:p0 + PC + 2048].rearrange("(t p) c -> p t c", p=128), writes=[t_kch])
                kb.dma("sp", qch[:, :, :], sc["zq"][g * 512:(g + 1) * 512, PAD + p0:PAD + p0 + PC].rearrange("(t p) c -> p t c", p=128), writes=[t_qch])
                nbb = PC // (128 * d)
                for r in range(d):
                    for bb in range(nbb):
                        q0 = r + d * 128 * bb
                        for o in range(2):
                            k0 = 1024 + r + d * (128 * (bb + o) - 64)
                            row0 = PAD + p0 + r + d * (128 * (bb + o) - 64)
                            vp = vpad[vi % 4]; tvp = t_vpad[vi % 4]; vi += 1
                            vsrc = zv[row0:row0 + 127 * d + 1:d, g * 512:(g + 1) * 512].rearrange("k (s two e) -> k s two e", two=2, e=64)
                            kb.dma("sp", vp[:, 0:8:2, 0:64], vsrc[:, :, 0, :], writes=[tvp])
                            kb.dma("sp", vp[:, 1:8:2, 64:128], vsrc[:, :, 1, :], writes=[tvp])
                            MODE = int(os.environ.get('ATT_MODE', '9'))
                            for s in range(8 if MODE >= 2 else 0):
                                pr = 64 * (s % 2); t = s // 2
                                kb.op("pe", lambda e: e.matmul(psS[o][:, SI(s) * 128:(SI(s) + 1) * 128], kch[pr:pr + 64, t, k0:k0 + 127 * d + 1:d], qch[pr:pr + 64, t, q0:q0 + 127 * d + 1:d], start=True, stop=True),
                                      reads=[t_kch, t_qch], writes=[t_psS[o]], inc=(s == 7))
                            if MODE < 2:
                                if o == 0:
                                    vp0, tvp0 = vp, tvp
                                else:
                                    vp1, tvp1 = vp, tvp
                                continue
                            SUB = int(os.environ.get('ATT_SUB', '9'))
                            if SUB >= 2:
                              kb.op("dve", lambda e: e.tensor_tensor(sbt[o][:, :], psS[o], biasT[:, g, o, :, :].rearrange("p s q -> p (s q)"), ALU.add), reads=[t_psS[o], t_biasT], writes=[t_sbt[o]])
                            if SUB >= 3:
                              kb.op("act", lambda e: e.activation(pT[o][:, :, :].rearrange("p s q -> p (s q)"), sbt[o][:, :], AF.Exp), reads=[t_sbt[o]], writes=[t_pT[o]])
                            first = (ch == 0 and bb == 0 and o == 0)
                            last = (p0 + PC == T and bb == nbb - 1 and o == 1)
                            if first and SUB >= 4:
                                kb.op("dve", lambda e: e.memset(pT[o][0:64, :, :], 0.0), writes=[t_pT[o]])
                            if last and SUB >= 4:
                                kb.op("dve", lambda e: e.memset(pT[o][64:128, :, :], 0.0), writes=[t_pT[o]])
                            if o == 0:
                                vp0, tvp0 = vp, tvp
                            else:
                                vp1, tvp1 = vp, tvp
                        vps = [(vp0, tvp0), (vp1, tvp1)]
                        if MODE < 3:
                            continue
                        for t in range(4):
                            n = 0
                            for s in (2 * t, 2 * t + 1):
                                for o in range(2):
                                    kb.op("pe", lambda e: e.matmul(psO[:, t * 128:(t + 1) * 128], vps[o][0][:, s, :], pT[o][:, SI(s), :], start=(n == 0), stop=(n == 3)),
                                          reads=[vps[o][1], t_pT[o]], writes=[t_psO], inc=False)
                                    n += 1
                            n = 0
                            for s in (2 * t, 2 * t + 1):
                                for o in range(2):
                                    kb.op("pe", lambda e: e.matmul(psZ[:, t * 128:(t + 1) * 128], onesp[:, s % 2, :], pT[o][:, SI(s), :], start=(n == 0), stop=(n == 3)),
                                          reads=[t_onesp, t_pT[o]], writes=[t_psZ, t_psO], inc=(n == 3 and t == 3))
                                    n += 1
                        tgO = accO[:, :, q0:q0 + 127 * d + 1:d]; tgZ = accZ[:, :, q0:q0 + 127 * d + 1:d]
                        pO3 = psO[:, :].rearrange("p (t q) -> p t q", t=4); pZ3 = psZ[:, :].rearrange("p (t q) -> p t q", t=4)
                        if g == 0:
                            kb.op("act", lambda e: e.copy(tgO, pO3), reads=[t_psO], writes=[t_accO])
                            kb.op("dve", lambda e: e.tensor_copy(tgZ, pZ3), reads=[t_psZ], writes=[t_accZ])
                        else:
                            kb.op("dve", lambda e: e.tensor_tensor(tgO, tgO, pO3, ALU.add), reads=[t_psO, t_accO], writes=[t_accO])
                            kb.op("dve", lambda e: e.tensor_tensor(tgZ, tgZ, pZ3, ALU.add), reads=[t_psZ, t_accZ], writes=[t_accZ])
            kb.op("dve", lambda e: e.reciprocal(accZ[:, :, :], accZ[:, :, :]), reads=[t_accZ], writes=[t_accZ])
            kb.op("dve", lambda e: e.tensor_tensor(yo[:, :, :], accO[:, :, :], accZ[:, :, :], ALU.mult), reads=[t_accO, t_accZ], writes=[t_yo])
            kb.dma("sp", sc["yatt"][:, p0:p0 + PC].rearrange("(t p) c -> p t c", p=128), yo[:, :, :], reads=[t_yo])
        kb.barrier()
        kb.es = old


def zero_pads(kb, env, sc, T):
    es = ExitStack()
    with es:
        old = kb.es; kb.es = es
        z = kb.sb([128, 1536], BF16); tz = Trk()
        kb.op("dve", lambda e: e.memset(z[:, :], 0.0), writes=[tz])
        zf = kb.sb([128, 4], F32); tzf = Trk()
        kb.op("dve", lambda e: e.memset(zf[:, :], 0.0), writes=[tzf])
        for side in (0, PAD + T):
            for t in range(12):
                kb.dma("sp", sc["zq"][t * 128:(t + 1) * 128, side:side + PAD], z[:, 0:PAD], reads=[tz])
                kb.dma("sp", sc["zk"][t * 128:(t + 1) * 128, side:side + PAD], z[:, 0:PAD], reads=[tz])
            for rb in range(PAD // 128):
                kb.dma("sp", sc["zv"][side + rb * 128:side + (rb + 1) * 128, :], z[:, :], reads=[tz])
        for t in range(8):
            kb.dma("sp", sc["zxl"][t * 128:(t + 1) * 128, 0:2], zf[:, 0:2], reads=[tzf])
            kb.dma("sp", sc["zxl"][t * 128:(t + 1) * 128, T + 2:T + 4], zf[:, 0:2], reads=[tzf])
        kb.barrier()
        kb.es = old


def make_scratch(nc, T, tag):
    import os
    dbg = os.environ.get("DBG_SCRATCH") == "1"
    dt = lambda n, s, d: nc.dram_tensor(f"{tag}_{n}", s, d, kind="ExternalOutput") if dbg else nc.dram_tensor(f"{tag}_{n}", s, d)
    sc = {"xT": [dt("xT0", [D, T], F32), dt("xT1", [D, T], F32)],
          "zxl": dt("zxl", [LW, T + 4], F32), "zgl": dt("zgl", [LW, T], BF16), "zu": dt("zu", [LW, T], BF16),
          "zq": dt("zq", [1536, T + 2 * PAD], BF16), "zk": dt("zk", [1536, T + 2 * PAD], BF16),
          "zv": dt("zv", [T + 2 * PAD, 1536], BF16),
          "zg": [dt(f"zg{i}", [D, T], BF16) for i in range(3)],
          "hf": dt("hf", [LW, T], F32), "ylru": dt("ylru", [LW, T], BF16),
          "yf": dt("yf", [LW, T], F32), "y1": dt("y1", [LW, T], BF16), "ys5": dt("ys5", [LW, T], BF16),
          "yatt": dt("yatt", [512, T], BF16),
          "tab": dt("tab", [64, 2, 128, NB + 4], F32),
          "Gd": dt("Gd", [24, 384], F32), "t_Gd": Trk(), "t_hf": Trk(), "t_yf": Trk()}
    return sc


def s5_full(kb, env, T, sc, cd, l):
    es = ExitStack()
    with es:
        old = kb.es; kb.es = es
        for nm in ("s5_rho", "s5_f", "s5_cr", "s5_ci", "s5_ncr", "s5_nci"):
            setattr(env, nm, kb.sb([128, 64], F32)); setattr(env, "t_" + nm, Trk())
        env.s5_Bw = kb.sb([128, 128, 128], BF16); env.t_s5_Bw = Trk()
        env.s5_Cw = kb.sb([128, 128, 128], BF16); env.t_s5_Cw = Trk()
        s5_prep(kb, env, cd, l, sc)
        s5_stage(kb, env, T, sc, l)
        kb.es = old


def lru_full(kb, env, T, sc, cd, l):
    es = ExitStack()
    with es:
        old = kb.es; kb.es = es
        lru_prep(kb, env, cd, l)
        lru_stage(kb, env, T, sc)
        kb.es = old


def trunk(kb, env, x_tok, y_tok, T, W, Wb, sc, cd, depth, stages=99):
    zero_pads(kb, env, sc, T)
    transpose_in(kb, env, x_tok, sc["xT"][0], T)
    cur = 0
    for l in range(depth):
        g0 = l * 96
        ffn_stage(kb, env, sc["xT"][cur], sc["xT"][1 - cur], T, Wb["w1"][l][0], Wb["w3"][l][0], Wb["w2"][l][0], g0, g0 + 16)
        cur = 1 - cur
        if stages < 2: break
        mixer_in_stage(kb, env, sc["xT"][cur], T, (Wb["win"][l], Wb["wv"][l]), sc, g0 + 32)
        if stages < 3: break
        lru_full(kb, env, T, sc, cd, l)
        if stages < 4: break
        s5_full(kb, env, T, sc, cd, l)
        glu_stage(kb, env, T, sc, W["s5_glu_w"][l], l)
        if stages < 5: break
        attn_stage(kb, env, T, sc, cd)
        if stages < 6: break
        merge_stage(kb, env, sc["xT"][cur], sc["xT"][1 - cur], T, sc, Wb["wbr"][l], None, None, Wb["wo"][l], g0 + 48)
        cur = 1 - cur
        ffn_stage(kb, env, sc["xT"][cur], sc["xT"][1 - cur], T, Wb["w1"][l][1], Wb["w3"][l][1], Wb["w2"][l][1], g0 + 64, g0 + 80)
        cur = 1 - cur
    transpose_out(kb, env, sc["xT"][cur], y_tok, T)


def precast_weights(kb, nc, W, depth):
    Wb = {"w1": [], "w3": [], "w2": [], "win": [], "wv": [], "wbr": [], "wo": []}
    for l in range(depth):
        w1l, w3l, w2l = [], [], []
        for i in range(2):
            a = nc.dram_tensor(f"b_w1_{l}_{i}", [JT, 128, KT, 128], BF16)
            b = nc.dram_tensor(f"b_w3_{l}_{i}", [JT, 128, KT, 128], BF16)
            c = nc.dram_tensor(f"b_w2_{l}_{i}", [KT, 128, JT, 128], BF16)
            for j in range(JT):
                kb.dma("pool", a[j], W["ffn_w1"][l, i][:, j * 128:(j + 1) * 128].rearrange("(k p) c -> p k c", p=128))
                kb.dma("pool", b[j], W["ffn_w3"][l, i][:, j * 128:(j + 1) * 128].rearrange("(k p) c -> p k c", p=128))
            for m in range(KT):
                kb.dma("pool", c[m], W["ffn_w2"][l, i][:, m * 128:(m + 1) * 128].rearrange("(j p) c -> p j c", p=128))
            w1l.append(a); w3l.append(b); w2l.append(c)
        Wb["w1"].append(w1l); Wb["w3"].append(w3l); Wb["w2"].append(w2l)
        wi = nc.dram_tensor(f"b_win_{l}", [108, 128, KT, 128], BF16)
        for ct in list(range(0, 48)) + list(range(60, 108)):
            kb.dma("pool", wi[ct], W["w_in"][l][:, ct * 128:(ct + 1) * 128].rearrange("(k p) c -> p k c", p=128))
        wv = nc.dram_tensor(f"b_wv_{l}", [3, 128, KT, 512], BF16)
        for vc in range(3):
            kb.dma("pool", wv[vc], W["w_in"][l][:, 48 * 128 + vc * 512:48 * 128 + (vc + 1) * 512].rearrange("(k p) c -> p k c", p=128))
        Wb["win"].append(wi); Wb["wv"].append(wv)
        wbr = nc.dram_tensor(f"b_wbr_{l}", [KT, 128, 20, 128], BF16)
        wo = nc.dram_tensor(f"b_wo_{l}", [KT, 128, KT, 128], BF16)
        for m in range(KT):
            kb.dma("pool", wbr[m, :, 0:8, :], W["w_br_lru"][l][:, m * 128:(m + 1) * 128].rearrange("(k p) c -> p k c", p=128))
            kb.dma("pool", wbr[m, :, 8:16, :], W["w_br_s5"][l][:, m * 128:(m + 1) * 128].rearrange("(k p) c -> p k c", p=128))
            kb.dma("pool", wbr[m, :, 16:20, :], W["w_br_att"][l][:, m * 128:(m + 1) * 128].rearrange("(k p) c -> p k c", p=128))
            kb.dma("pool", wo[m], W["w_out"][l][:, m * 128:(m + 1) * 128].rearrange("(k p) c -> p k c", p=128))
        Wb["wbr"].append(wbr); Wb["wo"].append(wo)
    kb.barrier()
    return Wb


L_ = 2
WNAMES = {"w_in": [L_, D, 13824], "ffn_w1": [L_, 2, D, DFF], "ffn_w3": [L_, 2, D, DFF], "ffn_w2": [L_, 2, DFF, D],
          "s5_glu_w": [L_, LW, LW], "w_br_lru": [L_, LW, D], "w_br_s5": [L_, LW, D], "w_br_att": [L_, 512, D], "w_out": [L_, D, D]}
CNAMES = {"ident": [128, 128], "gcol": [128, L_ * 96], "iota": [128, NB + 4],
          "lru_cw": [128, L_ * 32], "lru_cb": [128, L_ * 8], "lru_L": [128, L_ * 16], "lru_ba": [128, L_ * 16], "lru_bx": [128, L_ * 16],
          "lru_w": [L_, 128, 32, 128],
          "s5_lr": [128, L_ * 64], "s5_li": [128, L_ * 64], "s5_ldt": [128, L_ * 64],
          "s5_B": [L_, 128, 128, 128], "s5_C": [L_, 64, 2, 128, 128], "s5_d": [128, L_ * 8], "glu_b": [128, L_ * 8],
          "rel_bias": [32, 24], "antiI": [128, 128], "oh": [32, 3 * 384], "negm": [8, 384]}


def t5_buckets(rel):
    half = 16; max_exact = 8
    sign = (rel > 0).astype(np.int32) * half
    n = np.abs(rel)
    large = max_exact + (np.log(np.maximum(n, 1) / max_exact) / np.log(1024 / max_exact) * (half - max_exact)).astype(np.int32)
    large = np.minimum(large, half - 1)
    return sign + np.where(n < max_exact, n, large)


def host_consts(p):
    f32 = np.float32
    c = {}
    c["ident"] = np.eye(128, dtype=f32)
    c["antiI"] = np.eye(128, dtype=f32)[::-1].copy()
    c["iota"] = np.broadcast_to(np.arange(NB + 4, dtype=f32)[None, :], (128, NB + 4)).copy()
    c["gcol"] = p["norm_g"].reshape(L_, 6, 16, 128).transpose(3, 0, 1, 2).reshape(128, L_ * 96).copy()
    c["lru_cw"] = p["lru_conv_w"].reshape(L_, 4, 8, 128).transpose(3, 0, 1, 2).reshape(128, L_ * 32).copy()
    c["lru_cb"] = p["lru_conv_b"].reshape(L_, 8, 128).transpose(2, 0, 1).reshape(128, L_ * 8).copy()
    for nm, src in (("lru_L", "lru_L"), ("lru_ba", "lru_ba"), ("lru_bx", "lru_bx")):
        c[nm] = p[src].reshape(L_, 2, 8, 128).transpose(3, 0, 1, 2).reshape(128, L_ * 16).copy()
    lw = np.zeros((L_, 128, 32, 128), f32)
    for d in range(2):
        for j in range(8):
            for ax, nm in enumerate(("lru_wa", "lru_wx")):
                wi = (d * 8 + j) * 2 + ax
                for b in range(2):
                    lw[:, b * 64:(b + 1) * 64, wi, b * 64:(b + 1) * 64] = p[nm][:, d, 2 * j + b]
    c["lru_w"] = lw
    def s5v(a):
        return a.reshape(L_, 2, 32, 2, 64).transpose(3, 4, 0, 1, 2).reshape(128, L_ * 64).copy()
    c["s5_lr"] = s5v(p["s5_lam_re"]); c["s5_li"] = s5v(p["s5_lam_im"])
    c["s5_ldt"] = s5v(np.broadcast_to(p["s5_log_dt"][..., None], (L_, 2, 64, 64)))
    B = np.zeros((L_, 128, 128, 128), f32)
    C = np.zeros((L_, 64, 2, 128, 128), f32)
    for d in range(2):
        for i in range(32):
            col = d * 32 + i
            for gl in range(2):
                g = 2 * i + gl
                r0 = 16 * (g % 8)
                B[:, r0:r0 + 16, col * 2 + 0, gl * 64:(gl + 1) * 64] = p["s5_b_re"][:, d, g].transpose(0, 2, 1)
                B[:, r0:r0 + 16, col * 2 + 1, gl * 64:(gl + 1) * 64] = p["s5_b_im"][:, d, g].transpose(0, 2, 1)
                C[:, col, 0, gl * 64:(gl + 1) * 64, r0:r0 + 16] = p["s5_c_re"][:, d, g].transpose(0, 2, 1)
                C[:, col, 1, gl * 64:(gl + 1) * 64, r0:r0 + 16] = p["s5_c_im"][:, d, g].transpose(0, 2, 1)
    c["s5_B"] = B; c["s5_C"] = C
    c["s5_d"] = p["s5_d"].reshape(L_, 8, 128).transpose(2, 0, 1).reshape(128, L_ * 8).copy()
    c["glu_b"] = p["s5_glu_b"].reshape(L_, 8, 128).transpose(2, 0, 1).reshape(128, L_ * 8).copy()
    c["rel_bias"] = np.ascontiguousarray(p["rel_bias"], dtype=f32)
    oh = np.zeros((32, 3 * 384), f32); negm = np.full((8, 384), NEG, f32)
    for g in range(3):
        for n in range(383):
            j = (382 - n) - 191
            if abs(j) <= 64:
                b = int(t5_buckets(np.array([DIL[g] * j]))[0])
                oh[b, g * 384 + n] = 1.0
                negm[:, n] = 0.0
    c["oh"] = oh; c["negm"] = negm
    return {k: np.ascontiguousarray(v, dtype=f32) for k, v in c.items()}


def build_program(seqs, depth, stages=99, dbg=()):
    nc = bass.Bass("TRN2", target_bir_lowering=False)
    W = {n: nc.dram_tensor(n, s, F32, kind="ExternalInput") for n, s in WNAMES.items()}
    cd = {n: nc.dram_tensor("c_" + n, s, F32, kind="ExternalInput") for n, s in CNAMES.items()}
    xs = {n: nc.dram_tensor("x_" + n, [T, D], F32, kind="ExternalInput") for n, T in seqs}
    ys = {n: nc.dram_tensor("y_" + n, [T, D], F32, kind="ExternalOutput") for n, T in seqs}
    Tmax = max(T for _, T in seqs)
    es = ExitStack()
    with es:
        kb = KB(nc, es)
        env = Env()
        env.eps = kb.sb([128, 1], F32, "eps"); env.t_eps = Trk()
        kb.op("dve", lambda e: e.memset(env.eps[:, :], 1e-6), writes=[env.t_eps])
        env.one = kb.sb([128, 1], F32, "one"); env.t_one = Trk()
        kb.op("dve", lambda e: e.memset(env.one[:, :], 1.0), writes=[env.t_one])
        load_consts(kb, env, cd)
        env.iota = kb.sb([128, NB + 4], F32, "iota"); env.t_iota = Trk()
        kb.dma("sp", env.iota[:, :], cd["iota"][:, :], writes=[env.t_iota])
        env.s5_d = kb.sb([128, L_ * 8], F32, "s5d"); env.t_s5_d = Trk()
        kb.dma("sp", env.s5_d[:, :], cd["s5_d"][:, :], writes=[env.t_s5_d])
        env.glu_b = kb.sb([128, L_ * 8], F32, "glub"); env.t_glu_b = Trk()
        kb.dma("sp", env.glu_b[:, :], cd["glu_b"][:, :], writes=[env.t_glu_b])
        Wb = precast_weights(kb, nc, W, depth)
        for n, T in seqs:
            sc = make_scratch(nc, T, n)
            trunk(kb, env, xs[n], ys[n], T, W, Wb, sc, cd, depth, stages)
        kb.barrier()
    return nc


TA_, TB_ = 16384, 2048
NCORES = 4
_PROG = {}


def kernel(**inputs):
    p = {k: np.asarray(v) for k, v in inputs.items()}
    if "nc" not in _PROG:
        _PROG["nc"] = build_program([("A", TA_), ("B", TB_)], 2)
    nc = _PROG["nc"]
    consts = host_consts(p)
    base = {n: np.ascontiguousarray(p[n], dtype=np.float32) for n in WNAMES}
    base.update({"c_" + k: v for k, v in consts.items()})
    in_maps = []
    for c in range(NCORES):
        m = dict(base)
        m["x_A"] = np.ascontiguousarray(p["x_sample"][0]) if c == 0 else np.zeros((TA_, D), np.float32)
        m["x_B"] = np.ascontiguousarray(p["x_prompt"][c])
        in_maps.append(m)
    res = run_bass_kernel_spmd(nc, in_maps, core_ids=list(range(NCORES)))
    y_prompt = np.stack([res.results[c]["y_B"] for c in range(NCORES)], axis=0).astype(np.float32)
    y_sample = res.results[0]["y_A"][None].astype(np.float32)
    return (y_prompt, y_sample)
```
